# Optimizing a Trainium2 kernel written in Bass

```python
import math
import jax, jax.numpy as jnp
from jax import lax
import numpy as np

D_MODEL = 4096
BATCH = 1
SEQ = 8192
DEPTH = 2

CHUNK = 128
RMS_EPS = 1e-6
E_A = D_MODEL
P_A = 64
H_A = E_A // P_A
G_A = 8
N_A = 128
K_A = 4
CONV_A = E_A + 2 * G_A * N_A
E_B = D_MODEL // 2
POOL_WINDOWS = (2, 4, 8, 16)
N_POOL = len(POOL_WINDOWS)
C_POOL = E_B // N_POOL
DK_C = 128
DV_C = 128
H_C = (D_MODEL // 2) // DV_C
E_C = H_C * DV_C
ROPE_BASE = 10000.0
N_BRANCH = 3
D_FF = 256 * ((int(2 * 4 * D_MODEL / 3) + 255) // 256)
K_F = 3
IN_SIZES = (E_A, CONV_A, H_A, E_B, H_C * DK_C, H_C * DK_C, E_C, E_C, N_BRANCH * D_MODEL)
N_IN = sum(IN_SIZES)

kernel_name = "hybrid_ssd_pool_retention_parallel_gated"


def rms_norm(x, g, eps=RMS_EPS):
    xf = x.astype(jnp.float32)
    y = xf * lax.rsqrt(jnp.mean(xf * xf, axis=-1, keepdims=True) + eps)
    return (y * g.astype(jnp.float32)).astype(x.dtype)


def causal_dwconv(u, w, b):
    K = w.shape[0]
    L = u.shape[1]
    up = jnp.pad(u, ((0, 0), (K - 1, 0), (0, 0)))
    y = b
    for k in range(K):
        y = y + up[:, k:k + L, :] * w[k]
    return y


def rotary(x, pos):
    half = x.shape[-1] // 2
    inv = ROPE_BASE ** (-jnp.arange(half, dtype=jnp.float32) / half)
    ang = pos.astype(jnp.float32)[..., None] * inv
    cos = jnp.cos(ang)[:, :, None, :]
    sin = jnp.sin(ang)[:, :, None, :]
    x1, x2 = x[..., :half], x[..., half:]
    return jnp.concatenate([x1 * cos - x2 * sin, x1 * sin + x2 * cos], axis=-1)


def ssd_chunked(xs, Bm, Cm, dt, a_log, d_skip):
    b, L, H, P = xs.shape
    G, N = Bm.shape[2], Bm.shape[3]
    R = H // G
    c = L // CHUNK
    xf = xs.astype(jnp.float32).reshape(b, c, CHUNK, G, R, P)
    Bf = Bm.astype(jnp.float32).reshape(b, c, CHUNK, G, N)
    Cf = Cm.astype(jnp.float32).reshape(b, c, CHUNK, G, N)
    dtc = dt.reshape(b, c, CHUNK, G, R)
    A = -jnp.exp(a_log.astype(jnp.float32)).reshape(G, R)
    cs = jnp.cumsum(dtc * A, axis=2)
    xdt = xf * dtc[..., None]
    causal = jnp.tril(jnp.ones((CHUNK, CHUNK), dtype=bool))
    seg = cs[:, :, :, None] - cs[:, :, None, :]
    decay_ls = jnp.exp(jnp.where(causal[:, :, None, None], seg, -jnp.inf))
    cb = jnp.einsum('bclgn,bcsgn->bclsg', Cf, Bf)
    y_diag = jnp.einsum('bclsgr,bcsgrp->bclgrp', cb[..., None] * decay_ls, xdt)
    decay_to_end = jnp.exp(cs[:, :, -1:] - cs)
    chunk_states = jnp.einsum('bclgn,bclgrp->bcgrpn', Bf, xdt * decay_to_end[..., None])
    chunk_decay = jnp.exp(cs[:, :, -1])

    def step(state, inp):
        dec, st = inp
        return state * dec[..., None, None] + st, state

    _, prev = lax.scan(step, jnp.zeros((b, G, R, P, N), jnp.float32),
                       (jnp.moveaxis(chunk_decay, 1, 0), jnp.moveaxis(chunk_states, 1, 0)))
    prev = jnp.moveaxis(prev, 0, 1)
    y_off = jnp.einsum('bclgn,bcgrpn->bclgrp', Cf, prev) * jnp.exp(cs)[..., None]
    y = y_diag + y_off + xf * d_skip.astype(jnp.float32).reshape(G, R)[:, :, None]
    return y.reshape(b, L, H * P)


def retention_chunkwise(q, k, v):
    b, L, H, Dk = q.shape
    Dv = v.shape[-1]
    c = L // CHUNK
    lg = jnp.log1p(-jnp.exp2(-5.0 - jnp.arange(H, dtype=jnp.float32)))
    idx = jnp.arange(CHUNK, dtype=jnp.float32)
    causal = jnp.tril(jnp.ones((CHUNK, CHUNK), dtype=bool))
    rel = jnp.where(causal, idx[:, None] - idx[None, :], 0.0)
    Dmat = jnp.where(causal[None], jnp.exp(rel[None] * lg[:, None, None]), 0.0)
    qc = q.reshape(b, c, CHUNK, H, Dk)
    kc = k.reshape(b, c, CHUNK, H, Dk)
    vc = v.reshape(b, c, CHUNK, H, Dv)
    scores = jnp.einsum('bclhd,bcshd->bchls', qc, kc) * Dmat
    y_in = jnp.einsum('bchls,bcshe->bclhe', scores, vc)
    k_dec = kc * jnp.exp((CHUNK - 1 - idx)[:, None] * lg)[:, :, None]
    chunk_kv = jnp.einsum('bcshd,bcshe->bchde', k_dec, vc)
    chunk_decay = jnp.exp(CHUNK * lg)

    def step(state, kv):
        return state * chunk_decay[:, None, None] + kv, state

    _, prev = lax.scan(step, jnp.zeros((b, H, Dk, Dv), jnp.float32), jnp.moveaxis(chunk_kv, 1, 0))
    prev = jnp.moveaxis(prev, 0, 1)
    q_dec = qc * jnp.exp((idx + 1.0)[:, None] * lg)[:, :, None]
    y_cross = jnp.einsum('bclhd,bchde->bclhe', q_dec, prev)
    return (y_in + y_cross).reshape(b, L, H, Dv)


def multiscale_pool(u, pool_w, pool_scale):
    b, L, C = u.shape
    uf = u.astype(jnp.float32)
    cs0 = jnp.pad(jnp.cumsum(uf, axis=1), ((0, 0), (1, 0), (0, 0)))
    t = jnp.arange(L)
    outs = []
    for gi, w in enumerate(POOL_WINDOWS):
        c_g = cs0[:, :, gi * C_POOL:(gi + 1) * C_POOL]
        lag = jnp.pad(c_g, ((0, 0), (w - 1, 0), (0, 0)))[:, :L]
        cnt = jnp.minimum(t + 1, w).astype(jnp.float32)
        mean = (c_g[:, 1:] - lag) / cnt[None, :, None]
        outs.append(mean - uf[:, :, gi * C_POOL:(gi + 1) * C_POOL])
    pooled = jnp.stack(outs, axis=2).astype(u.dtype)
    mixed = jnp.einsum('blgc,gcd->blgd', pooled, pool_w)
    return mixed.reshape(b, L, C) * pool_scale


def setup_inputs(seed: int = 0) -> dict:
    key = jax.random.key(seed)
    ks = jax.random.split(key, 32)

    def nrm(k, shape, scale):
        return jax.random.normal(k, shape, jnp.float32) * scale

    def gain(k, shape):
        return 1.0 + 0.02 * jax.random.normal(k, shape, jnp.float32)

    x = jax.random.normal(ks[0], (BATCH, SEQ, D_MODEL), jnp.float32)
    start = jax.random.randint(ks[1], (BATCH, 1), 0, 4096, dtype=jnp.int32)
    positions = start + jnp.arange(SEQ, dtype=jnp.int32)[None, :]
    dt0 = jnp.exp(jax.random.uniform(ks[8], (DEPTH, H_A), jnp.float32)
                  * (math.log(0.1) - math.log(1e-3)) + math.log(1e-3))
    dt_bias = dt0 + jnp.log(-jnp.expm1(-dt0))
    a_log = jnp.log(jax.random.uniform(ks[9], (DEPTH, H_A), jnp.float32, 1.0, 16.0))
    return {
        "x": x,
        "positions": positions,
        "norm_mix": gain(ks[2], (DEPTH, D_MODEL)),
        "w_in": nrm(ks[3], (DEPTH, D_MODEL, N_IN), D_MODEL ** -0.5),
        "b_gate": nrm(ks[4], (DEPTH, N_BRANCH * D_MODEL), 0.1),
        "conv_a_w": nrm(ks[5], (DEPTH, K_A, CONV_A), K_A ** -0.5),
        "conv_a_b": nrm(ks[6], (DEPTH, CONV_A), 0.01),
        "dt_bias": dt_bias,
        "a_log": a_log,
        "d_skip": gain(ks[10], (DEPTH, H_A)),
        "norm_a": gain(ks[11], (DEPTH, E_A)),
        "pool_w": nrm(ks[12], (DEPTH, N_POOL, C_POOL, C_POOL), C_POOL ** -0.5),
        "pool_scale": gain(ks[13], (DEPTH, E_B)),
        "norm_c": gain(ks[14], (DEPTH, DV_C)),
        "w_br_a": nrm(ks[15], (DEPTH, E_A, D_MODEL), E_A ** -0.5),
        "w_br_b": nrm(ks[16], (DEPTH, E_B, D_MODEL), E_B ** -0.5),
        "w_br_c": nrm(ks[17], (DEPTH, E_C, D_MODEL), E_C ** -0.5),
        "w_out": nrm(ks[18], (DEPTH, D_MODEL, D_MODEL), D_MODEL ** -0.5),
        "norm_ffn": gain(ks[19], (DEPTH, D_MODEL)),
        "w_up": nrm(ks[20], (DEPTH, D_MODEL, 2 * D_FF), D_MODEL ** -0.5),
        "conv_f_w": nrm(ks[21], (DEPTH, K_F, 2 * D_FF), K_F ** -0.5),
        "conv_f_b": nrm(ks[22], (DEPTH, 2 * D_FF), 0.01),
        "w_down": nrm(ks[23], (DEPTH, D_FF, D_MODEL), D_FF ** -0.5),
        "norm_f": gain(ks[24], (D_MODEL,)),
    }


def reference(x, positions, norm_mix, w_in, b_gate, conv_a_w, conv_a_b, dt_bias, a_log,
              d_skip, norm_a, pool_w, pool_scale, norm_c, w_br_a, w_br_b, w_br_c, w_out,
              norm_ffn, w_up, conv_f_w, conv_f_b, w_down, norm_f):
    b, L, _ = x.shape
    offs = np.cumsum((0,) + IN_SIZES).tolist()
    for i in range(DEPTH):
        h = rms_norm(x, norm_mix[i])
        proj = h @ w_in[i]
        z_a, xbc_a, dt_a, u_b, q_c, k_c, v_c, g_c, gate_raw = [
            proj[..., offs[j]:offs[j + 1]] for j in range(len(IN_SIZES))]

        xbc = jax.nn.silu(causal_dwconv(xbc_a, conv_a_w[i], conv_a_b[i]))
        xs = xbc[..., :E_A].reshape(b, L, H_A, P_A)
        Bm = xbc[..., E_A:E_A + G_A * N_A].reshape(b, L, G_A, N_A)
        Cm = xbc[..., E_A + G_A * N_A:].reshape(b, L, G_A, N_A)
        dt = jax.nn.softplus((dt_a + dt_bias[i]).astype(jnp.float32))
        y_ssd = ssd_chunked(xs, Bm, Cm, dt, a_log[i], d_skip[i])
        yz = (y_ssd * jax.nn.silu(z_a.astype(jnp.float32))).reshape(b, L, G_A, E_A // G_A)
        yz = yz * lax.rsqrt(jnp.mean(yz * yz, axis=-1, keepdims=True) + RMS_EPS)
        y_a = (yz.reshape(b, L, E_A) * norm_a[i].astype(jnp.float32)).astype(x.dtype)

        y_b = multiscale_pool(u_b, pool_w[i], pool_scale[i]).astype(x.dtype)

        q = rotary(q_c.astype(jnp.float32).reshape(b, L, H_C, DK_C), positions) * (DK_C ** -0.5)
        k = rotary(k_c.astype(jnp.float32).reshape(b, L, H_C, DK_C), positions)
        v = v_c.astype(jnp.float32).reshape(b, L, H_C, DV_C)
        ret = retention_chunkwise(q, k, v)
        ret = ret * lax.rsqrt(jnp.mean(ret * ret, axis=-1, keepdims=True) + RMS_EPS)
        ret = ret * norm_c[i].astype(jnp.float32)
        y_c = (ret.reshape(b, L, E_C) * jax.nn.silu(g_c.astype(jnp.float32))).astype(x.dtype)

        gates = jax.nn.sigmoid((gate_raw + b_gate[i]).astype(jnp.float32)).reshape(b, L, N_BRANCH, D_MODEL)
        merged = (gates[:, :, 0] * (y_a @ w_br_a[i]).astype(jnp.float32)
                  + gates[:, :, 1] * (y_b @ w_br_b[i]).astype(jnp.float32)
                  + gates[:, :, 2] * (y_c @ w_br_c[i]).astype(jnp.float32)).astype(x.dtype)
        x = x + merged @ w_out[i]

        h = rms_norm(x, norm_ffn[i])
        up = causal_dwconv(h @ w_up[i], conv_f_w[i], conv_f_b[i])
        act = jax.nn.silu(up[..., :D_FF]) * up[..., D_FF:]
        x = x + act @ w_down[i]
    return rms_norm(x, norm_f)
```

```python
import numpy as np
import concourse.bass as bass
import concourse.mybir as mybir
from concourse.bass_utils import run_bass_kernel_spmd
from contextlib import ExitStack

F32 = mybir.dt.float32
BF16 = mybir.dt.bfloat16
I32 = mybir.dt.int32
AF = mybir.ActivationFunctionType
ALU = mybir.AluOpType
AX = mybir.AxisListType

NCORES = 8
L = 8192
D = 4096
TT = 512
NT = L // TT
DEPTH = 2
EPS = 1e-6
DFF = 11008
FPC = 1408
NJF = 11
ENG = ("pe", "act", "dve", "pool", "sp")


class LazyIn:
    def __init__(self, nc, name, shape, dt, used):
        self.nc, self.name, self.shape, self.dt, self.used = nc, name, shape, dt, used
        self._ap = None

    @property
    def A(self):
        if self._ap is None:
            self._ap = self.nc.dram_tensor(self.name, list(self.shape), self.dt, kind="ExternalInput").ap()
            self.used.append(self.name)
        return self._ap

    def __getitem__(self, k):
        return self.A[k]


class _Stop(Exception):
    pass


class Reg:
    def __init__(self, ap, multi=False):
        self._ap = ap
        self.w = {}
        self.r = {}
        self.multi = multi

    @property
    def ap(self):
        if isinstance(self._ap, LazyIn):
            return self._ap.A
        return self._ap

    def __getitem__(self, k):
        return self.ap[k]


class _Rec:
    def __init__(self):
        self.calls = []

    def __getattr__(self, name):
        def f(*a, **k):
            self.calls.append((name, a, k))
            return self
        return f


class Prog:
    def __init__(self, nc):
        self.nc = nc
        self.ops = {e: [] for e in ENG}
        self.cnt = {}
        self.seen = {e: {} for e in ENG}

    def op(self, eng, fn, reads=(), writes=(), dma=None, amt=None):
        need = {}

        def add(k, v):
            if need.get(k, 0) < v:
                need[k] = v
        for r in reads:
            for k, v in r.w.items():
                add(k, v)
        for w in writes:
            if not w.multi:
                for k, v in w.w.items():
                    add(k, v)
            for k, v in w.r.items():
                add(k, v)
        waits = []
        for k, v in need.items():
            if self.seen[eng].get(k, 0) < v:
                self.seen[eng][k] = v
                waits.append((k, v))
        key = dma if dma is not None else eng
        if dma is not None and self.cnt.get(key, 0) > self.seen[eng].get(key, 0):
            self.seen[eng][key] = self.cnt[key]
            waits.append((key, self.cnt[key]))
        a = amt if amt is not None else (16 if dma is not None else 1)
        self.cnt[key] = self.cnt.get(key, 0) + a
        val = self.cnt[key]
        rec = _Rec()
        fn(rec)
        assert rec.calls
        self.ops[eng].append((waits, rec.calls, key, a))
        for w in writes:
            if w.multi:
                w.w[key] = val
            else:
                w.w = {key: val}
                w.r = {}
        for r in reads:
            if r.r.get(key, 0) < val:
                r.r[key] = val

    def barrier(self, final=False):
        for e in ENG:
            waits = []
            for k, v in self.cnt.items():
                if k == "cc" and not final:
                    continue
                if self.seen[e].get(k, 0) < v:
                    self.seen[e][k] = v
                    waits.append((k, v))
            if waits:
                self.ops[e].append((waits, None, None, 0))

    def emit(self, es):
        nc = self.nc
        sems = {k: es.enter_context(nc.semaphore("s_" + k)) for k in self.cnt}
        block = es.enter_context(nc.Block())
        deco = {"pe": block.tensor, "act": block.scalar, "dve": block.vector,
                "pool": block.gpsimd, "sp": block.sync}
        for e in ENG:
            ops = self.ops[e]

            def body(eng, ops=ops):
                for waits, fn, key, a in ops:
                    for k, v in waits:
                        eng.wait_ge(sems[k], v)
                    if fn is not None:
                        for name, ca, ck_ in fn:
                            ins = getattr(eng, name)(*ca, **ck_)
                        if key == "cc":
                            ins.then_inc(sems[key])
                        else:
                            ins.then_inc(sems[key], a)
            deco[e](body)


def bcast(ap, axis, n):
    a = ap.unsqueeze(axis)
    shp = list(a.shape)
    shp[axis] = n
    return a.broadcast_to(shp)


class Arena:
    def __init__(self, t, n):
        self.t = t
        self.n = n
        self.off = 0

    def reset(self):
        self.off = 0

    def get(self, *shape, multi=False):
        n = int(np.prod(shape))
        n_al = (n + 15) // 16 * 16
        assert self.off + n_al <= self.n, (self.off, n_al, self.n)
        ap = self.t[:, self.off:self.off + n]
        self.off += n_al
        if len(shape) == 2:
            ap = ap.rearrange("p (a b) -> p a b", a=shape[0])
        elif len(shape) == 3:
            ap = ap.rearrange("p (a b c) -> p a b c", a=shape[0], b=shape[1])
        return Reg(ap)


NBF = 70400
NF32 = 10496


def build(DEPTH=DEPTH, stop=None, dump=None, probe=None):
    nc = bass.Bass("TRN2", target_bir_lowering=False)
    P = Prog(nc)
    used = []
    nc._used_inputs = used

    def din(name, shape, dt=F32):
        return LazyIn(nc, name, shape, dt, used)

    def dscr(name, shape, dt):
        return Reg(nc.dram_tensor(name, list(shape), dt).ap(), multi=True)

    xs = din("xs", [512, L])
    posr = din("posr", [128, L], I32)
    cst = din("cst", [128, 5, 128])
    invf = din("invf", [128, 1])
    retc = din("retc", [128, 2 * 128 + 2 * 128 + 2 + 2])
    poolc = din("poolc", [128, 4 + 512 + 512])
    w1 = din("w1", [DEPTH, D, 1280])
    wdt = din("wdt", [DEPTH, D, 8])
    w2 = din("w2", [DEPTH, D, 1024])
    w2p = din("w2p", [DEPTH, D, 512])
    wpool = din("wpool", [DEPTH, 512, 256])
    w3 = din("w3", [DEPTH, D, 1536])
    pcat = din("pcat", [DEPTH, 8192, 512])
    w4 = din("w4", [DEPTH, D, 512])
    w5 = din("w5", [DEPTH, D, 2 * FPC])
    w6 = din("w6", [DEPTH, NJF * 8 * 128, 512])
    pp = din("pp", [DEPTH, 128, 64])
    pb = din("pb", [DEPTH, 128, 8 + 8 + 8 + 512 + 128])
    nf = din("nf", [128, 4])
    pf = din("pf", [DEPTH, 128, 88])
    if stop is None:
        out = nc.dram_tensor("out", [512, L], F32, kind="ExternalOutput").ap()
        outR = Reg(out, multi=True)

    xcur = dscr("xcur", [512, L], F32)
    xmid = dscr("xmid", [512, L], F32)
    ssq_in = dscr("ssq_in", [1, L], F32)
    ssq_all = dscr("ssq_all", [8, L], F32)
    NP = 4
    LH = L // NP

    def dpair(name, rows, dt):
        return [dscr(f"{name}_{i}", [rows, LH], dt) for i in range(NP)]
    h_in = dpair("h_in", 512, BF16)
    hT = dpair("hT", D, BF16)
    hlo_in = dscr("hlo_in", [512, 128], BF16)
    hloT = dscr("hloT", [D, 128], BF16)
    y_in = dpair("y_in", 1024, BF16)
    yT = dpair("yT", 8192, BF16)
    mg_in = dpair("mg_in", 512, BF16)
    mgT = dpair("mgT", D, BF16)
    a_in = dpair("a_in", FPC, BF16)
    aT = dpair("aT", FPC * 8, BF16)
    HT = NT // NP
    xsR = Reg(xs, multi=True)

    with ExitStack() as es:
        def sb(name, shape, dt):
            return es.enter_context(nc.sbuf_tensor(name, list(shape), dt))
        abf_t = sb("abf", [128, NBF], BF16)
        af_t = sb("af32", [128, NF32], F32)
        AB = Arena(abf_t, NBF)
        AFL = Arena(af_t, NF32)
        cst_t = Reg(sb("cst_t", [128, 5, 128], F32)[:])
        identb = Reg(sb("identb", [128, 128], BF16)[:])
        invf_t = Reg(sb("invf_t", [128, 1], F32)[:])
        retc_t = Reg(sb("retc_t", [128, 516], F32)[:])
        poolc_t = Reg(sb("poolc_t", [128, 1028], F32)[:])
        pp_t = Reg(sb("pp_t", [128, 64], F32)[:])
        pb_t = Reg(sb("pb_t", [128, 664], F32)[:])
        nf_t = Reg(sb("nf_t", [128, 4], F32)[:])
        pf_t = Reg(sb("pf_t", [128, 88], F32)[:])
        prevS = Reg(sb("prevS", [128, 512], F32)[:])
        prevSb = Reg(sb("prevSb", [128, 512], BF16)[:])
        prevR = Reg(sb("prevR", [128, 256], F32)[:])
        prevRb = Reg(sb("prevRb", [128, 256], BF16)[:])
        Aneg = Reg(sb("Aneg", [128, 8], F32)[:])
        psf = [Reg(es.enter_context(nc.psum_tensor(f"ps{i}", [128, 512], F32))[:]) for i in range(7)]
        psb = Reg(es.enter_context(nc.psum_tensor("psb", [128, 1024], BF16))[:])
        ident = cst_t.ap[:, 0, :]
        ones = cst_t.ap[:, 1, :]
        triT = cst_t.ap[:, 2, :]
        negm = cst_t.ap[:, 3, :]
        Rm = cst_t.ap[:, 4, :]
        dmaN = [0]

        def dma(outR_, out_ap, inR_, in_ap, eng="sp"):
            key = f"d{eng}{dmaN[0] % 6}"
            dmaN[0] += 1
            if isinstance(in_ap, LazyIn):
                in_ap = in_ap.A
            P.op(eng, lambda e: e.dma_start(out=out_ap, in_=in_ap), reads=[inR_], writes=[outR_], dma=key)

        cin = Reg(None, multi=True)
        dma(cst_t, cst_t.ap, cin, cst)
        dma(invf_t, invf_t.ap, cin, invf)
        dma(retc_t, retc_t.ap, cin, retc)
        dma(poolc_t, poolc_t.ap, cin, poolc)
        dma(nf_t, nf_t.ap, cin, nf)
        P.op("dve", lambda e: e.tensor_copy(out=identb.ap, in_=ident), reads=[cst_t], writes=[identb])

        def allgather(src, dst):
            P.op("pool", lambda e: e.collective_compute(
                "AllGather", ALU.bypass, replica_groups=[list(range(NCORES))],
                ins=[src.ap.opt()], outs=[dst.ap.opt()]), reads=[src], writes=[dst], dma="cc", amt=1)

        def mm(ps, ps_ap, pairs, reads, fp32=False):
            def fn(e):
                n = len(pairs)
                for i, (a, b) in enumerate(pairs):
                    ins = e.matmul(ps_ap, a, b, start=(i == 0), stop=(i == n - 1))
                return ins
            P.op("pe", fn, reads=reads, writes=[ps])

        def tr(ps, ps_ap, in_ap, idn, reads):
            P.op("pe", lambda e: e.transpose(ps_ap, in_ap, idn), reads=reads, writes=[ps])

        def V(fn, reads, writes):
            P.op("dve", fn, reads=reads, writes=writes)

        def A(fn, reads, writes):
            P.op("act", fn, reads=reads, writes=writes)

        def load_w(dst, dst_ap, src_ap):
            dma(dst, dst_ap, cin, src_ap, eng="pool")

        def wview(w_ap, kc):
            return w_ap.rearrange("(kc p) n -> p kc n", p=128)

        def act_view(t, kc, tt):
            u = tt % HT
            return t[tt // HT].ap.rearrange("(kc p) t -> p kc t", p=128)[:, :, u * TT:(u + 1) * TT]

        def gather_half(src, dst, tt):
            if (tt + 1) % HT == 0:
                allgather(src[tt // HT], dst[tt // HT])

        def stream_tiles(srcs, TS, body, gather=None):
            NS = L // TS
            HS = NS // NP

            def load(st):
                pc, u = st // HS, st % HS
                for pieces, kc, bufs in srcs:
                    b = bufs[st % 2]
                    dma(b, b.ap, pieces[pc], pieces[pc].ap.rearrange("(kc p) t -> p kc t", p=128)[:, :, u * TS:(u + 1) * TS])
            for st in range(NS):
                if st == 0:
                    load(0)
                if st + 1 < NS:
                    load(st + 1)
                body(st, [bufs[st % 2] for _, _, bufs in srcs], st // HS, (st % HS) * TS)
                if gather is not None and (st + 1) % HS == 0:
                    allgather(gather[0][st // HS], gather[1][st // HS])

        def rmsnorm_gather(xsrc, gain_ap, final=False, want_lo=False):
            P.barrier()
            AB.reset(); AFL.reset()
            xt = [AFL.get(4, TT) for _ in range(2)]
            sq = AFL.get(4, TT)
            row = AFL.get(TT)
            s8 = AFL.get(TT)
            rstd = AFL.get(TT)
            hb = [AB.get(4, TT) for _ in range(2)]
            hf32 = AFL.get(4, 128)
            hlo = AB.get(4, 128)
            xv = xsrc.ap.rearrange("(b p) t -> p b t", p=128)
            for tt in range(NT):
                x_ = xt[tt % 2]
                dma(x_, x_.ap, xsrc, xv[:, :, tt * TT:(tt + 1) * TT])
                A(lambda e, x_=x_: e.activation(out=sq.ap, in_=x_.ap, func=AF.Square), [x_], [sq])
                mm(psf[0], psf[0].ap, [(ones, sq.ap[:, b, :]) for b in range(4)], [sq, cst_t])
                V(lambda e: e.tensor_copy(out=row.ap[0:1, :], in_=psf[0].ap[0:1, :]), [psf[0]], [row])
                dma(ssq_in, ssq_in.ap[:, tt * TT:(tt + 1) * TT], row, row.ap[0:1, :])
            allgather(ssq_in, ssq_all)
            for tt in range(NT):
                x_ = xt[tt % 2]
                dma(s8, s8.ap[0:8, :], ssq_all, ssq_all.ap[:, tt * TT:(tt + 1) * TT])
                dma(x_, x_.ap, xsrc, xv[:, :, tt * TT:(tt + 1) * TT])
                mm(psf[0], psf[0].ap, [(ones[0:8, :], s8.ap[0:8, :])], [s8, cst_t])
                A(lambda e: e.activation(out=rstd.ap, in_=psf[0].ap, func=AF.Sqrt, scale=1.0 / D, bias=EPS),
                  [psf[0]], [rstd])
                V(lambda e: e.reciprocal(out=rstd.ap, in_=rstd.ap), [rstd], [rstd])
                if not final:
                    h_ = hb[tt % 2]
                    for b in range(4):
                        V(lambda e, b=b, x_=x_, h_=h_: e.scalar_tensor_tensor(
                            out=h_.ap[:, b, :], in0=x_.ap[:, b, :], scalar=gain_ap[:, b:b + 1],
                            in1=rstd.ap, op0=ALU.mult, op1=ALU.mult), [x_, rstd, pp_t, nf_t], [h_])
                    dma(h_in[tt // HT], h_in[tt // HT].ap.rearrange("(b p) t -> p b t", p=128)[:, :, (tt % HT) * TT:(tt % HT + 1) * TT], h_, h_.ap)
                    gather_half(h_in, hT, tt)
                    if want_lo and tt == 0:
                        for b in range(4):
                            V(lambda e, b=b, x_=x_: e.scalar_tensor_tensor(
                                out=hf32.ap[:, b, :], in0=x_.ap[:, b, 0:128], scalar=gain_ap[:, b:b + 1],
                                in1=rstd.ap[:, 0:128], op0=ALU.mult, op1=ALU.mult), [x_, rstd, pp_t, nf_t], [hf32])
                        V(lambda e, h_=h_: e.tensor_tensor(out=hlo.ap, in0=hf32.ap, in1=h_.ap[:, :, 0:128], op=ALU.subtract),
                          [hf32, h_], [hlo])
                        dma(hlo_in, hlo_in.ap.rearrange("(b p) t -> p b t", p=128), hlo, hlo.ap)
                else:
                    o_ = x_
                    for b in range(4):
                        V(lambda e, b=b, x_=x_, o_=o_: e.scalar_tensor_tensor(
                            out=o_.ap[:, b, :], in0=x_.ap[:, b, :], scalar=gain_ap[:, b:b + 1],
                            in1=rstd.ap, op0=ALU.mult, op1=ALU.mult), [x_, rstd, pp_t, nf_t], [o_])
                    dma(outR, out.rearrange("(b p) t -> p b t", p=128)[:, :, tt * TT:(tt + 1) * TT], o_, o_.ap)
            if not final:
                if want_lo:
                    allgather(hlo_in, hloT)

        def conv_fm(dst_ap, pre, H, K, wcol, tmp):
            for k in range(K):
                src = pre.ap[:, H - (K - 1) + k: H - (K - 1) + k + TT]
                if k == 0:
                    V(lambda e, src=src, k=k: e.tensor_scalar(out=tmp.ap, in0=src, scalar1=wcol(k), scalar2=None,
                                                               op0=ALU.mult), [pre, pp_t], [tmp])
                else:
                    V(lambda e, src=src, k=k: e.scalar_tensor_tensor(out=tmp.ap, in0=src, scalar=wcol(k), in1=tmp.ap,
                                                                      op0=ALU.mult, op1=ALU.add), [pre, pp_t, tmp], [tmp])

        def ckpt(name):
            if stop == name:
                raise _Stop()

        def sbdump(tag, reg, ap=None):
            ap = reg.ap if ap is None else ap
            t = nc.dram_tensor("pr_" + tag, list(ap.shape), ap.dtype, kind="ExternalOutput").ap()
            dma(Reg(t, multi=True), t, reg, ap)

        try:
          for li in range(DEPTH):
            P.barrier()
            dma(pp_t, pp_t.ap, cin, pp[li])
            dma(pb_t, pb_t.ap, cin, pb[li])
            dma(pf_t, pf_t.ap, cin, pf[li])
            rmsnorm_gather(xsR if li == 0 else xcur, pp_t.ap[:, 0:4], want_lo=True)
            ckpt("s0")

            P.barrier()
            AB.reset(); AFL.reset()
            W1 = AB.get(32, 1280)
            Wd = AB.get(32, 8)
            hA = AB.get(32, TT)
            BT = AB.get(TT); CT = AB.get(TT)
            MT = AB.get(8, 128); Cs = AB.get(8, 128)
            xdt = AB.get(TT); xdt2 = AB.get(TT); ya = AB.get(TT)
            yafm = AB.get(4, TT); Btm = AB.get(128)
            pre = [AFL.get(3 + TT) for _ in range(6)]
            xc = [AFL.get(TT) for _ in range(4)]
            ctmp = AFL.get(TT)
            zs = AFL.get(TT); xtm = AFL.get(TT)
            big1 = AFL.get(8, 128); big2 = AFL.get(8, 128)
            t5 = AFL.get(TT); yv = AFL.get(TT); junk = AFL.get(TT)
            sm = AFL.get(16, 8)
            ssq1 = AFL.get(4)
            for c in range(5):
                load_w(W1, W1.ap[:, :, c * 256:(c + 1) * 256], wview(w1[li], 32)[:, :, c * 256:(c + 1) * 256])
            load_w(Wd, Wd.ap, wview(wdt[li], 32))
            dtb = pb_t.ap[:, 0:8]; alog = pb_t.ap[:, 8:16]; dsk = pb_t.ap[:, 16:24]
            norma = pb_t.ap[:, 24:536]
            A(lambda e: e.activation(out=Aneg.ap, in_=alog, func=AF.Exp), [pb_t], [Aneg])
            V(lambda e: e.tensor_scalar(out=Aneg.ap, in0=Aneg.ap, scalar1=-1.0, scalar2=None, op0=ALU.mult), [Aneg], [Aneg])
            V(lambda e: e.memset(prevS.ap, 0.0), [], [prevS])
            V(lambda e: e.memset(prevSb.ap, 0.0), [], [prevSb])
            for r_ in pre:
                V(lambda e, r_=r_: e.memset(r_.ap, 0.0), [], [r_])
            S = lambda i: sm.ap[:, i, :]
            for tt in range(NT):
                dma(hA, hA.ap, hT[tt // HT], act_view(hT, 32, tt))
                for b in range(6):
                    c0 = 512 + b * 128
                    ps = psf[b % 2]
                    mm(ps, ps.ap, [(W1.ap[:, k, c0:c0 + 128], hA.ap[:, k, :]) for k in range(32)], [W1, hA])
                    A(lambda e, b=b, ps=ps: e.activation(out=pre[b].ap[:, 3:3 + TT], in_=ps.ap, func=AF.Copy), [ps], [pre[b]])
                    conv_fm(None, pre[b], 3, 4, lambda k, b=b: pp_t.ap[:, 8 + b * 4 + k: 9 + b * 4 + k], ctmp)
                    dst = xc[b] if b < 4 else (BT if b == 4 else CT)
                    A(lambda e, b=b, dst=dst: e.activation(out=dst.ap, in_=ctmp.ap, func=AF.Silu,
                                                            bias=pp_t.ap[:, 32 + b:33 + b], scale=1.0), [ctmp, pp_t], [dst])
                    V(lambda e, b=b: e.tensor_copy(out=pre[b].ap[:, 0:3], in_=pre[b].ap[:, TT:TT + 3]), [pre[b]], [pre[b]])
                for c in range(4):
                    ck = slice(c * 128, (c + 1) * 128)
                    mm(psf[2], psf[2].ap, [(hA.ap[:, k, ck], W1.ap[:, k, 0:512]) for k in range(32)], [W1, hA])
                    A(lambda e: e.activation(out=zs.ap, in_=psf[2].ap, func=AF.Silu), [psf[2]], [zs])
                    mm(psf[3], psf[3].ap[:, 0:8], [(hA.ap[:, k, ck], Wd.ap[:, k, :]) for k in range(32)], [Wd, hA])
                    xx, ax, ee, dt_, a_, cs_, dte, cd = S(0), S(1), S(2), S(3), S(4), S(5), S(6), S(7)
                    V(lambda e: e.tensor_tensor(out=xx, in0=psf[3].ap[:, 0:8], in1=dtb, op=ALU.add), [psf[3], pb_t], [sm])
                    V(lambda e: e.scalar_tensor_tensor(out=ax, in0=xx, scalar=-1.0, in1=xx, op0=ALU.mult, op1=ALU.max), [sm], [sm])
                    A(lambda e: e.activation(out=ee, in_=ax, func=AF.Exp, scale=-1.0), [sm], [sm])
                    A(lambda e: e.activation(out=ee, in_=ee, func=AF.Ln, bias=1.0, scale=1.0), [sm], [sm])
                    V(lambda e: e.scalar_tensor_tensor(out=dt_, in0=xx, scalar=0.0, in1=ee, op0=ALU.max, op1=ALU.add), [sm], [sm])
                    V(lambda e: e.tensor_tensor(out=a_, in0=dt_, in1=Aneg.ap, op=ALU.mult), [sm, Aneg], [sm])
                    mm(psf[3], psf[3].ap[:, 8:16], [(triT, a_)], [sm, cst_t])
                    V(lambda e: e.tensor_copy(out=cs_, in_=psf[3].ap[:, 8:16]), [psf[3]], [sm])
                    V(lambda e: e.tensor_tensor(out=big1.ap, in0=bcast(triT, 1, 8), in1=bcast(a_, 2, 128), op=ALU.mult),
                      [sm, cst_t], [big1])
                    b1f = big1.ap.rearrange("p h l -> p (h l)")
                    mm(psf[4], psf[4].ap, [(ones, b1f[:, 0:512])], [big1, cst_t])
                    mm(psf[5], psf[5].ap, [(ones, b1f[:, 512:1024])], [big1, cst_t])
                    csB = [psf[4].ap.rearrange("p (h l) -> p h l", h=4), psf[5].ap.rearrange("p (h l) -> p h l", h=4)]
                    for hf in range(2):
                        V(lambda e, hf=hf: e.tensor_tensor(out=dte[:, hf * 4:hf * 4 + 4], in0=csB[hf][:, :, 127],
                                                           in1=cs_[:, hf * 4:hf * 4 + 4], op=ALU.subtract), [psf[4 + hf], sm], [sm])
                        A(lambda e, hf=hf: e.activation(out=cd[:, hf * 4:hf * 4 + 4], in_=csB[hf][:, :, 127], func=AF.Exp),
                          [psf[4 + hf]], [sm])
                    A(lambda e: e.activation(out=dte, in_=dte, func=AF.Exp), [sm], [sm])
                    V(lambda e: e.tensor_tensor(out=dte, in0=dte, in1=dt_, op=ALU.mult), [sm], [sm])
                    for hf in range(2):
                        hs = slice(hf * 4, hf * 4 + 4)
                        V(lambda e, hf=hf, hs=hs: e.tensor_tensor(out=big2.ap[:, hs, :], in0=csB[hf],
                                                                 in1=bcast(cs_[:, hs], 2, 128), op=ALU.subtract),
                          [psf[4 + hf], sm], [big2])
                    V(lambda e: e.tensor_tensor(out=big2.ap, in0=big2.ap, in1=bcast(negm, 1, 8), op=ALU.add), [big2, cst_t], [big2])
                    A(lambda e: e.activation(out=big2.ap, in_=big2.ap, func=AF.Exp), [big2], [big2])
                    for hf in range(2):
                        hs = slice(hf * 4, hf * 4 + 4)
                        A(lambda e, hf=hf, hs=hs: e.activation(out=big1.ap[:, hs, :], in_=csB[hf], func=AF.Exp),
                          [psf[4 + hf]], [big1])
                    mm(psf[6], psf[6].ap[:, 0:128], [(BT.ap[:, ck], CT.ap[:, ck])], [BT, CT])
                    V(lambda e: e.tensor_tensor(out=MT.ap, in0=big2.ap, in1=bcast(psf[6].ap[:, 0:128], 1, 8), op=ALU.mult),
                      [big2, psf[6]], [MT])
                    V(lambda e: e.tensor_tensor(out=Cs.ap, in0=big1.ap, in1=bcast(CT.ap[:, ck], 1, 8), op=ALU.mult),
                      [big1, CT], [Cs])
                    for b in range(4):
                        tr(psf[2], psf[2].ap[:, b * 128:(b + 1) * 128], xc[b].ap[:, ck], ident, [xc[b], cst_t])
                    V(lambda e: e.tensor_copy(out=xtm.ap, in_=psf[2].ap), [psf[2]], [xtm])
                    tr(psb, psb.ap[:, 0:128], BT.ap[:, ck], identb.ap, [BT, identb])
                    V(lambda e: e.tensor_copy(out=Btm.ap, in_=psb.ap[:, 0:128]), [psb], [Btm])
                    x3 = xtm.ap.rearrange("p (h q) -> p h q", h=8)
                    V(lambda e: e.tensor_tensor(out=xdt.ap.rearrange("p (h q) -> p h q", h=8), in0=x3,
                                                in1=bcast(dt_, 2, 64), op=ALU.mult), [xtm, sm], [xdt])
                    V(lambda e: e.tensor_tensor(out=xdt2.ap.rearrange("p (h q) -> p h q", h=8), in0=x3,
                                                in1=bcast(dte, 2, 64), op=ALU.mult), [xtm, sm], [xdt2])
                    def yfn(e):
                        for h in range(8):
                            hq = slice(h * 64, (h + 1) * 64)
                            e.matmul(psf[3].ap[:, hq], MT.ap[:, h, :], xdt.ap[:, hq], start=True, stop=False)
                            ins = e.matmul(psf[3].ap[:, hq], Cs.ap[:, h, :], prevSb.ap[:, hq], start=False, stop=True)
                        return ins
                    if probe == ("s1", tt, c):
                        sbdump("prevS", prevS); sbdump("prevSb", prevSb); sbdump("E2", big1); sbdump("LT", big2)
                        sbdump("Cs", Cs); sbdump("MT", MT); sbdump("xdt", xdt); sbdump("sm", sm); sbdump("xtm", xtm)
                    P.op("pe", yfn, reads=[MT, Cs, xdt, prevSb], writes=[psf[3]])
                    V(lambda e: e.tensor_tensor(out=t5.ap.rearrange("p (h q) -> p h q", h=8), in0=x3,
                                                in1=bcast(dsk, 2, 64), op=ALU.mult), [xtm, pb_t], [t5])
                    V(lambda e: e.tensor_tensor(out=yv.ap, in0=t5.ap, in1=psf[3].ap, op=ALU.add), [t5, psf[3]], [yv])
                    if probe == ("s1", tt, c):
                        sbdump("yv", yv)
                    V(lambda e: e.tensor_tensor(out=yv.ap, in0=yv.ap, in1=zs.ap, op=ALU.mult), [yv, zs], [yv])
                    A(lambda e: e.activation(out=junk.ap, in_=yv.ap, func=AF.Square, accum_out=ssq1.ap[:, 0:1]), [yv], [junk, ssq1])
                    A(lambda e: e.activation(out=ssq1.ap[:, 1:2], in_=ssq1.ap[:, 0:1], func=AF.Sqrt, scale=1.0 / 512, bias=EPS),
                      [ssq1], [ssq1])
                    V(lambda e: e.reciprocal(out=ssq1.ap[:, 2:3], in_=ssq1.ap[:, 1:2]), [ssq1], [ssq1])
                    V(lambda e: e.scalar_tensor_tensor(out=ya.ap, in0=yv.ap, scalar=ssq1.ap[:, 2:3], in1=norma,
                                                       op0=ALU.mult, op1=ALU.mult), [yv, ssq1, pb_t], [ya])
                    for b in range(4):
                        tr(psb, psb.ap[:, 128 + b * 128:256 + b * 128], ya.ap[:, b * 128:(b + 1) * 128], identb.ap, [ya, identb])
                    V(lambda e, ck=ck: e.tensor_copy(out=yafm.ap[:, :, ck],
                                                     in_=psb.ap[:, 128:640].rearrange("p (b t) -> p b t", b=4)), [psb], [yafm])
                    mm(psf[6], psf[6].ap, [(Btm.ap, xdt2.ap)], [Btm, xdt2])
                    V(lambda e: e.tensor_tensor(out=prevS.ap.rearrange("p (h q) -> p h q", h=8),
                                                in0=prevS.ap.rearrange("p (h q) -> p h q", h=8),
                                                in1=bcast(cd, 2, 64), op=ALU.mult), [prevS, sm, prevSb], [prevS])
                    V(lambda e: e.tensor_tensor(out=prevS.ap, in0=prevS.ap, in1=psf[6].ap, op=ALU.add), [prevS, psf[6]], [prevS])
                    V(lambda e: e.tensor_copy(out=prevSb.ap, in_=prevS.ap), [prevS], [prevSb])
                dma(y_in[tt // HT], y_in[tt // HT].ap[0:512, :].rearrange("(b p) t -> p b t", p=128)[:, :, (tt % HT) * TT:(tt % HT + 1) * TT], yafm, yafm.ap)

            ckpt("s1")
            P.barrier()
            AB.reset(); AFL.reset()
            W2 = AB.get(32, 1024)
            hA = AB.get(32, TT)
            qr = AB.get(2, TT); kr = AB.get(2, TT)
            vb = AB.get(256); kdt = AB.get(2, 128); STm = AB.get(2, 128); qd = AB.get(2, 128)
            yc = AB.get(256); ycfm = AB.get(2, TT)
            posi = es.enter_context(nc.sbuf_tensor(f"posi{li}", [128, TT], I32))
            posiR = Reg(posi[:])
            posf = AFL.get(TT); ang = AFL.get(TT); u_ = AFL.get(TT); k_ = u_; r_ = AFL.get(TT)
            cosT = AFL.get(TT); sinT = AFL.get(TT)
            qf = AFL.get(TT); t1 = AFL.get(TT); t2 = AFL.get(TT)
            gs = AFL.get(256); yt = AFL.get(256); sq2 = AFL.get(256); rs = AFL.get(8)
            wst = AFL.get(32, 128)
            wlo = AB.get(32, 128)
            hlo_t = AB.get(32, 128)
            qk32 = AFL.get(4, 128)
            for c in range(4):
                load_w(W2, W2.ap[:, :, c * 256:(c + 1) * 256], wview(w2[li], 32)[:, :, c * 256:(c + 1) * 256])
            DmatT = retc_t.ap[:, 0:256].rearrange("p (h l) -> p h l", h=2)
            qdec = retc_t.ap[:, 256:512].rearrange("p (h l) -> p h l", h=2)
            kdec = retc_t.ap[:, 512:514]
            cdec = retc_t.ap[:, 514:516]
            normc = pb_t.ap[:, 536:664]
            V(lambda e: e.memset(prevR.ap, 0.0), [], [prevR])
            V(lambda e: e.memset(prevRb.ap, 0.0), [], [prevRb])
            TWO_PI = 2.0 * np.pi
            C1 = 6.28125
            C2 = float(TWO_PI - C1)
            MAGIC = 12582912.0
            for tt in range(NT):
                dma(hA, hA.ap, hT[tt // HT], act_view(hT, 32, tt))
                dma(posiR, posiR.ap, cin, posr[:, tt * TT:(tt + 1) * TT])
                V(lambda e: e.tensor_copy(out=posf.ap, in_=posiR.ap), [posiR], [posf])
                V(lambda e: e.tensor_scalar(out=ang.ap, in0=posf.ap, scalar1=invf_t.ap[:, 0:1], scalar2=None, op0=ALU.mult),
                  [posf, invf_t], [ang])
                for which, dstT in ((0, sinT), (1, cosT)):
                    off = 0.25 if which else 0.0
                    V(lambda e, off=off: e.tensor_scalar(out=u_.ap, in0=ang.ap, scalar1=1.0 / TWO_PI, scalar2=off,
                                                         op0=ALU.mult, op1=ALU.add), [ang], [u_])
                    V(lambda e: e.tensor_scalar(out=k_.ap, in0=u_.ap, scalar1=MAGIC, scalar2=None, op0=ALU.add), [u_], [k_])
                    V(lambda e: e.tensor_scalar(out=k_.ap, in0=k_.ap, scalar1=MAGIC, scalar2=None, op0=ALU.subtract), [k_], [k_])
                    V(lambda e: e.scalar_tensor_tensor(out=r_.ap, in0=k_.ap, scalar=-C1, in1=ang.ap, op0=ALU.mult, op1=ALU.add),
                      [k_, ang], [r_])
                    V(lambda e: e.scalar_tensor_tensor(out=r_.ap, in0=k_.ap, scalar=-C2, in1=r_.ap, op0=ALU.mult, op1=ALU.add),
                      [k_, r_], [r_])
                    if which:
                        V(lambda e: e.tensor_scalar(out=r_.ap, in0=r_.ap, scalar1=float(np.pi / 2), scalar2=None, op0=ALU.add), [r_], [r_])
                    V(lambda e: e.tensor_scalar(out=r_.ap, in0=r_.ap, scalar1=3.1415925, scalar2=-3.1415925,
                                                op0=ALU.min, op1=ALU.max), [r_], [r_])
                    A(lambda e, dstT=dstT: e.activation(out=dstT.ap, in_=r_.ap, func=AF.Sin), [r_], [dstT])
                V(lambda e: e.tensor_scalar(out=sinT.ap[0:64, :], in0=sinT.ap[0:64, :], scalar1=-1.0, scalar2=None, op0=ALU.mult),
                  [sinT], [sinT])
                if tt == 0:
                    dma(hlo_t, hlo_t.ap, hloT, hloT.ap.rearrange("(kc p) t -> p kc t", p=128))
                for qk in range(2):
                    dstq = qr if qk == 0 else kr
                    for hh in range(2):
                        c0 = qk * 256 + hh * 128
                        mm(psf[0], psf[0].ap, [(W2.ap[:, k, c0:c0 + 128], hA.ap[:, k, :]) for k in range(32)], [W2, hA])
                        A(lambda e: e.activation(out=qf.ap, in_=psf[0].ap, func=AF.Copy), [psf[0]], [qf])
                        if tt == 0:
                            dma(wst, wst.ap, cin, wview(w2[li], 32)[:, :, c0:c0 + 128])
                            V(lambda e, c0=c0: e.tensor_tensor(out=wlo.ap, in0=wst.ap, in1=W2.ap[:, :, c0:c0 + 128], op=ALU.subtract),
                              [wst, W2], [wlo])
                            mm(psf[6], psf[6].ap[:, 0:128],
                               [(wlo.ap[:, k, :], hA.ap[:, k, 0:128]) for k in range(32)] +
                               [(W2.ap[:, k, c0:c0 + 128], hlo_t.ap[:, k, :]) for k in range(32)], [wlo, W2, hA, hlo_t])
                            V(lambda e: e.tensor_tensor(out=qf.ap[:, 0:128], in0=qf.ap[:, 0:128], in1=psf[6].ap[:, 0:128], op=ALU.add),
                              [qf, psf[6]], [qf])
                        mm(psf[1], psf[1].ap, [(Rm, qf.ap)], [qf, cst_t])
                        V(lambda e: e.tensor_tensor(out=t1.ap, in0=qf.ap, in1=cosT.ap, op=ALU.mult), [qf, cosT], [t1])
                        V(lambda e: e.tensor_tensor(out=t2.ap, in0=psf[1].ap, in1=sinT.ap, op=ALU.mult), [psf[1], sinT], [t2])
                        V(lambda e, dstq=dstq, hh=hh: e.tensor_tensor(out=dstq.ap[:, hh, :], in0=t1.ap, in1=t2.ap, op=ALU.add),
                          [t1, t2], [dstq])
                        if tt == 0:
                            V(lambda e, qk=qk, hh=hh: e.tensor_tensor(out=qk32.ap[:, qk * 2 + hh, :], in0=t1.ap[:, 0:128],
                                                                    in1=t2.ap[:, 0:128], op=ALU.add), [t1, t2], [qk32])
                for c in range(4):
                    ck = slice(c * 128, (c + 1) * 128)
                    mm(psf[2], psf[2].ap, [(hA.ap[:, k, ck], W2.ap[:, k, 512:1024]) for k in range(32)], [W2, hA])
                    A(lambda e: e.activation(out=vb.ap, in_=psf[2].ap[:, 0:256], func=AF.Copy), [psf[2]], [vb])
                    A(lambda e: e.activation(out=gs.ap, in_=psf[2].ap[:, 256:512], func=AF.Silu), [psf[2]], [gs])
                    for hh in range(2):
                        tr(psb, psb.ap[:, hh * 128:(hh + 1) * 128], kr.ap[:, hh, ck], identb.ap, [kr, identb])
                    for hh in range(2):
                        A(lambda e, hh=hh: e.activation(out=kdt.ap[:, hh, :], in_=psb.ap[:, hh * 128:(hh + 1) * 128],
                                                        func=AF.Copy, scale=kdec[:, hh:hh + 1]), [psb, retc_t], [kdt])
                    for hh in range(2):
                        if tt == 0 and c == 0:
                            mm(psf[3], psf[3].ap[:, hh * 128:(hh + 1) * 128], [(qk32.ap[:, 2 + hh, :], qk32.ap[:, hh, :])], [qk32])
                        else:
                            mm(psf[3], psf[3].ap[:, hh * 128:(hh + 1) * 128], [(kr.ap[:, hh, ck], qr.ap[:, hh, ck])], [kr, qr])
                    V(lambda e: e.tensor_tensor(out=STm.ap, in0=psf[3].ap[:, 0:256].rearrange("p (h l) -> p h l", h=2),
                                                in1=DmatT, op=ALU.mult), [psf[3], retc_t], [STm])
                    V(lambda e, ck=ck: e.tensor_tensor(out=qd.ap, in0=qr.ap[:, :, ck], in1=qdec, op=ALU.mult), [qr, retc_t], [qd])

                    def yfn2(e):
                        for hh in range(2):
                            hq = slice(hh * 128, (hh + 1) * 128)
                            e.matmul(psf[4].ap[:, hq], STm.ap[:, hh, :], vb.ap[:, hq], start=True, stop=False)
                            ins = e.matmul(psf[4].ap[:, hq], qd.ap[:, hh, :], prevRb.ap[:, hq], start=False, stop=True)
                        return ins
                    P.op("pe", yfn2, reads=[STm, qd, vb, prevRb], writes=[psf[4]])
                    y3 = psf[4].ap[:, 0:256].rearrange("p (h l) -> p h l", h=2)
                    V(lambda e: e.tensor_copy(out=yt.ap, in_=psf[4].ap[:, 0:256]), [psf[4]], [yt])
                    V(lambda e: e.tensor_tensor(out=sq2.ap, in0=yt.ap, in1=yt.ap, op=ALU.mult), [yt], [sq2])
                    V(lambda e: e.tensor_reduce(out=rs.ap[:, 0:2], in_=sq2.ap.rearrange("p (h l) -> p h l", h=2),
                                                axis=AX.X, op=ALU.add), [sq2], [rs])
                    A(lambda e: e.activation(out=rs.ap[:, 2:4], in_=rs.ap[:, 0:2], func=AF.Sqrt, scale=1.0 / 128, bias=EPS), [rs], [rs])
                    V(lambda e: e.reciprocal(out=rs.ap[:, 4:6], in_=rs.ap[:, 2:4]), [rs], [rs])
                    V(lambda e: e.tensor_tensor(out=yt.ap.rearrange("p (h l) -> p h l", h=2),
                                                in0=yt.ap.rearrange("p (h l) -> p h l", h=2),
                                                in1=bcast(rs.ap[:, 4:6], 2, 128), op=ALU.mult), [yt, rs], [yt])
                    V(lambda e: e.tensor_tensor(out=yt.ap.rearrange("p (h l) -> p h l", h=2),
                                                in0=yt.ap.rearrange("p (h l) -> p h l", h=2),
                                                in1=bcast(normc, 1, 2), op=ALU.mult), [yt, pb_t], [yt])
                    V(lambda e: e.tensor_tensor(out=yc.ap, in0=yt.ap, in1=gs.ap, op=ALU.mult), [yt, gs], [yc])
                    for hh in range(2):
                        tr(psb, psb.ap[:, 256 + hh * 128:384 + hh * 128], yc.ap[:, hh * 128:(hh + 1) * 128], identb.ap, [yc, identb])
                    V(lambda e, ck=ck: e.tensor_copy(out=ycfm.ap[:, :, ck],
                                                     in_=psb.ap[:, 256:512].rearrange("p (b t) -> p b t", b=2)), [psb], [ycfm])
                    for hh in range(2):
                        mm(psf[5], psf[5].ap[:, hh * 128:(hh + 1) * 128], [(kdt.ap[:, hh, :], vb.ap[:, hh * 128:(hh + 1) * 128])], [kdt, vb])
                    V(lambda e: e.tensor_tensor(out=prevR.ap.rearrange("p (h l) -> p h l", h=2),
                                                in0=prevR.ap.rearrange("p (h l) -> p h l", h=2),
                                                in1=bcast(cdec, 2, 128), op=ALU.mult), [prevR, retc_t, prevRb], [prevR])
                    V(lambda e: e.tensor_tensor(out=prevR.ap, in0=prevR.ap, in1=psf[5].ap[:, 0:256], op=ALU.add), [prevR, psf[5]], [prevR])
                    V(lambda e: e.tensor_copy(out=prevRb.ap, in_=prevR.ap), [prevR], [prevRb])
                dma(y_in[tt // HT], y_in[tt // HT].ap[768:1024, :].rearrange("(b p) t -> p b t", p=128)[:, :, (tt % HT) * TT:(tt % HT + 1) * TT], ycfm, ycfm.ap)

            ckpt("s2a")
            P.barrier()
            AB.reset(); AFL.reset()
            W2p = AB.get(32, 512)
            Wpl = AB.get(4, 256)
            hA = AB.get(32, TT)
            pooled = AB.get(4, TT)
            ybfm = AB.get(2, TT)
            HP = 15
            preu = [AFL.get(HP + TT) for _ in range(4)]
            s2 = AFL.get(HP + TT); s4 = AFL.get(HP + TT); s8_ = AFL.get(HP + TT); s16 = AFL.get(HP + TT)
            wsum = AFL.get(TT)
            for c in range(2):
                load_w(W2p, W2p.ap[:, :, c * 256:(c + 1) * 256], wview(w2p[li], 32)[:, :, c * 256:(c + 1) * 256])
            load_w(Wpl, Wpl.ap, wview(wpool[li], 4))
            for r2 in preu:
                V(lambda e, r2=r2: e.memset(r2.ap, 0.0), [], [r2])
            cw = lambda i: poolc_t.ap[:, i:i + 1]
            Wd_ = HP + TT
            for tt in range(NT):
                dma(hA, hA.ap, hT[tt // HT], act_view(hT, 32, tt))
                inv_ap = poolc_t.ap[:, 4:516] if tt == 0 else poolc_t.ap[:, 516:1028]
                for b in range(4):
                    ps = psf[b % 2]
                    u = preu[b]
                    mm(ps, ps.ap, [(W2p.ap[:, k, b * 128:(b + 1) * 128], hA.ap[:, k, :]) for k in range(32)], [W2p, hA])
                    A(lambda e, u=u, ps=ps: e.activation(out=u.ap[:, HP:], in_=ps.ap, func=AF.Copy), [ps], [u])
                    V(lambda e, u=u: e.tensor_tensor(out=s2.ap[:, 1:Wd_], in0=u.ap[:, 1:Wd_], in1=u.ap[:, 0:Wd_ - 1], op=ALU.add), [u], [s2])
                    V(lambda e: e.tensor_tensor(out=s4.ap[:, 3:Wd_], in0=s2.ap[:, 3:Wd_], in1=s2.ap[:, 1:Wd_ - 2], op=ALU.add), [s2], [s4])
                    V(lambda e: e.tensor_tensor(out=s8_.ap[:, 7:Wd_], in0=s4.ap[:, 7:Wd_], in1=s4.ap[:, 3:Wd_ - 4], op=ALU.add), [s4], [s8_])
                    V(lambda e: e.tensor_tensor(out=s16.ap[:, 15:Wd_], in0=s8_.ap[:, 15:Wd_], in1=s8_.ap[:, 7:Wd_ - 8], op=ALU.add), [s8_], [s16])
                    V(lambda e: e.tensor_scalar(out=wsum.ap, in0=s2.ap[:, HP:], scalar1=cw(0), scalar2=None, op0=ALU.mult), [s2, poolc_t], [wsum])
                    for i, sx in ((1, s4), (2, s8_), (3, s16)):
                        V(lambda e, i=i, sx=sx: e.scalar_tensor_tensor(out=wsum.ap, in0=sx.ap[:, HP:], scalar=cw(i), in1=wsum.ap,
                                                                        op0=ALU.mult, op1=ALU.add), [sx, wsum, poolc_t], [wsum])
                    V(lambda e, inv_ap=inv_ap: e.tensor_tensor(out=wsum.ap, in0=wsum.ap, in1=inv_ap, op=ALU.mult), [wsum, poolc_t], [wsum])
                    V(lambda e, u=u, b=b: e.tensor_tensor(out=pooled.ap[:, b, :], in0=wsum.ap, in1=u.ap[:, HP:], op=ALU.subtract),
                      [wsum, u], [pooled])
                    V(lambda e, u=u: e.tensor_copy(out=u.ap[:, 0:HP], in_=u.ap[:, TT:TT + HP]), [u], [u])
                for db in range(2):
                    ps = psf[2 + db]
                    mm(ps, ps.ap, [(Wpl.ap[:, k, db * 128:(db + 1) * 128], pooled.ap[:, k, :]) for k in range(4)], [Wpl, pooled])
                    A(lambda e, db=db, ps=ps: e.activation(out=ybfm.ap[:, db, :], in_=ps.ap, func=AF.Copy,
                                                           scale=pp_t.ap[:, 38 + db:39 + db]), [ps, pp_t], [ybfm])
                dma(y_in[tt // HT], y_in[tt // HT].ap[512:768, :].rearrange("(b p) t -> p b t", p=128)[:, :, (tt % HT) * TT:(tt % HT + 1) * TT], ybfm, ybfm.ap)
                gather_half(y_in, yT, tt)
            ckpt("s2b")

            for jb in range(4):
                P.barrier()
                AB.reset(); AFL.reset()
                TS = 256
                W3 = AB.get(32, 384)
                Pc = AB.get(64, 128)
                hA2 = [AB.get(32, TS) for _ in range(2)]
                yA2 = [AB.get(64, TS) for _ in range(2)]
                mg2 = [AB.get(TS) for _ in range(2)]
                sig = [AFL.get(TS) for _ in range(3)]
                acc = AFL.get(TS); tq = AFL.get(TS)
                for i in range(3):
                    load_w(W3, W3.ap[:, :, i * 128:(i + 1) * 128], wview(w3[li], 32)[:, :, i * 512 + jb * 128:i * 512 + (jb + 1) * 128])
                load_w(Pc, Pc.ap, wview(pcat[li], 64)[:, :, jb * 128:(jb + 1) * 128])
                sel = ([0, 1, 2, 3], [4, 5], [6, 7])

                def body3(st, bufs, pc, c0, jb=jb, W3=W3, Pc=Pc, mg2=mg2, sig=sig, acc=acc, tq=tq, TS=TS):
                    h_, y_ = bufs
                    mg = mg2[st % 2]
                    for i in range(3):
                        mm(psf[i], psf[i].ap[:, 0:TS], [(W3.ap[:, k, i * 128:(i + 1) * 128], h_.ap[:, k, :]) for k in range(32)], [W3, h_])
                        A(lambda e, i=i: e.activation(out=sig[i].ap, in_=psf[i].ap[:, 0:TS], func=AF.Sigmoid,
                                                      bias=pp_t.ap[:, 40 + i * 4 + jb:41 + i * 4 + jb], scale=1.0), [psf[i], pp_t], [sig[i]])
                    for i in range(3):
                        ch = [q * 8 + j for q in range(8) for j in sel[i]]
                        mm(psf[3 + i], psf[3 + i].ap[:, 0:TS], [(Pc.ap[:, k, :], y_.ap[:, k, :]) for k in ch], [Pc, y_])
                    V(lambda e: e.tensor_tensor(out=acc.ap, in0=sig[0].ap, in1=psf[3].ap[:, 0:TS], op=ALU.mult), [sig[0], psf[3]], [acc])
                    V(lambda e: e.tensor_tensor(out=tq.ap, in0=sig[1].ap, in1=psf[4].ap[:, 0:TS], op=ALU.mult), [sig[1], psf[4]], [tq])
                    V(lambda e: e.tensor_tensor(out=acc.ap, in0=acc.ap, in1=tq.ap, op=ALU.add), [acc, tq], [acc])
                    V(lambda e: e.tensor_tensor(out=tq.ap, in0=sig[2].ap, in1=psf[5].ap[:, 0:TS], op=ALU.mult), [sig[2], psf[5]], [tq])
                    V(lambda e: e.tensor_tensor(out=mg.ap, in0=acc.ap, in1=tq.ap, op=ALU.add), [acc, tq], [mg])
                    dma(mg_in[pc], mg_in[pc].ap[jb * 128:(jb + 1) * 128, c0:c0 + TS], mg, mg.ap)
                stream_tiles([(hT, 32, hA2), (yT, 64, yA2)], TS, body3, gather=(mg_in, mgT) if jb == 3 else None)

            ckpt("s3")
            P.barrier()
            AB.reset(); AFL.reset()
            W4 = AB.get(32, 512)
            mA2 = [AB.get(32, TT) for _ in range(2)]
            xt4 = [AFL.get(4, TT) for _ in range(2)]
            xo4 = [AFL.get(4, TT) for _ in range(2)]
            xsrc = xsR if li == 0 else xcur
            for c in range(2):
                load_w(W4, W4.ap[:, :, c * 256:(c + 1) * 256], wview(w4[li], 32)[:, :, c * 256:(c + 1) * 256])

            def body4(st, bufs, pc, c0, W4=W4, xt4=xt4, xo4=xo4, xsrc=xsrc):
                mA = bufs[0]
                xt, xo = xt4[st % 2], xo4[st % 2]
                dma(xt, xt.ap, xsrc, xsrc.ap.rearrange("(b p) t -> p b t", p=128)[:, :, st * TT:(st + 1) * TT])
                for jb in range(4):
                    ps = psf[jb % 2]
                    mm(ps, ps.ap, [(W4.ap[:, k, jb * 128:(jb + 1) * 128], mA.ap[:, k, :]) for k in range(32)], [W4, mA])
                    V(lambda e, jb=jb, ps=ps: e.tensor_tensor(out=xo.ap[:, jb, :], in0=xt.ap[:, jb, :], in1=ps.ap, op=ALU.add),
                      [xt, ps], [xo])
                dma(xmid, xmid.ap.rearrange("(b p) t -> p b t", p=128)[:, :, st * TT:(st + 1) * TT], xo, xo.ap)
            stream_tiles([(mgT, 32, mA2)], TT, body4)
            rmsnorm_gather(xmid, pp_t.ap[:, 4:8])

            ckpt("s4")
            carry = es.enter_context(nc.sbuf_tensor(f"carry{li}", [128, NJF * 2, 2], F32))
            carryR = Reg(carry[:])
            V(lambda e: e.memset(carryR.ap, 0.0), [], [carryR])
            for p0, p1 in ((0, 4), (4, 8), (8, 11)):
                P.barrier()
                AB.reset(); AFL.reset()
                npj = p1 - p0
                W5 = AB.get(32, npj * 256)
                hA2 = [AB.get(32, TT) for _ in range(2)]
                afm2 = [AB.get(TT) for _ in range(2)]
                prea = AFL.get(2 + TT); preb = AFL.get(2 + TT)
                ca = AFL.get(TT); cb = AFL.get(TT); sa = AFL.get(TT)
                for j in range(npj):
                    jf = p0 + j
                    load_w(W5, W5.ap[:, :, j * 256:j * 256 + 128], wview(w5[li], 32)[:, :, jf * 128:(jf + 1) * 128])
                    load_w(W5, W5.ap[:, :, j * 256 + 128:j * 256 + 256], wview(w5[li], 32)[:, :, FPC + jf * 128:FPC + (jf + 1) * 128])

                def body5(st, bufs, pc, c0, p0=p0, npj=npj, W5=W5, afm2=afm2, prea=prea, preb=preb, ca=ca, cb=cb, sa=sa):
                    hA = bufs[0]
                    for j in range(npj):
                        jf = p0 + j
                        afm = afm2[j % 2]
                        for ab, prex, cx in ((0, prea, ca), (1, preb, cb)):
                            ps = psf[ab]
                            cc0 = j * 256 + ab * 128
                            mm(ps, ps.ap, [(W5.ap[:, k, cc0:cc0 + 128], hA.ap[:, k, :]) for k in range(32)], [W5, hA])
                            V(lambda e: e.tensor_copy(out=prex.ap[:, 0:2], in_=carryR.ap[:, jf * 2 + ab, :]), [carryR], [prex])
                            A(lambda e: e.activation(out=prex.ap[:, 2:], in_=ps.ap, func=AF.Copy), [ps], [prex])
                            wc = lambda k: pf_t.ap[:, (jf * 2 + ab) * 4 + k:(jf * 2 + ab) * 4 + k + 1]
                            for k in range(3):
                                src = prex.ap[:, k:k + TT]
                                if k == 0:
                                    V(lambda e: e.tensor_scalar(out=cx.ap, in0=src, scalar1=wc(k), scalar2=None, op0=ALU.mult), [prex, pf_t], [cx])
                                else:
                                    V(lambda e: e.scalar_tensor_tensor(out=cx.ap, in0=src, scalar=wc(k), in1=cx.ap,
                                                                       op0=ALU.mult, op1=ALU.add), [prex, pf_t, cx], [cx])
                            V(lambda e: e.tensor_copy(out=carryR.ap[:, jf * 2 + ab, :], in_=prex.ap[:, TT:TT + 2]), [prex], [carryR])
                        A(lambda e: e.activation(out=sa.ap, in_=ca.ap, func=AF.Silu, bias=pf_t.ap[:, (jf * 2) * 4 + 3:(jf * 2) * 4 + 4],
                                                 scale=1.0), [ca, pf_t], [sa])
                        V(lambda e: e.scalar_tensor_tensor(out=afm.ap, in0=cb.ap, scalar=pf_t.ap[:, (jf * 2 + 1) * 4 + 3:(jf * 2 + 1) * 4 + 4],
                                                           in1=sa.ap, op0=ALU.add, op1=ALU.mult), [cb, sa, pf_t], [afm])
                        dma(a_in[pc], a_in[pc].ap[jf * 128:(jf + 1) * 128, c0:c0 + TT], afm, afm.ap)
                stream_tiles([(hT, 32, hA2)], TT, body5, gather=(a_in, aT) if p1 == NJF else None)

            ckpt("s5")
            for hp in range(2):
                P.barrier()
                AB.reset(); AFL.reset()
                TS = 256
                W6 = AB.get(88, 256)
                aA2 = [AB.get(88, TS) for _ in range(2)]
                xt6 = [AFL.get(2, TS) for _ in range(2)]
                xo6 = [AFL.get(2, TS) for _ in range(2)]
                for c in range(4):
                    load_w(W6, W6.ap[:, c * 22:(c + 1) * 22, :], wview(w6[li], 88)[:, c * 22:(c + 1) * 22, hp * 256:(hp + 1) * 256])

                def body6(st, bufs, pc, c0, hp=hp, W6=W6, xt6=xt6, xo6=xo6, TS=TS):
                    aA = bufs[0]
                    xt, xo = xt6[st % 2], xo6[st % 2]
                    dma(xt, xt.ap, xmid, xmid.ap[hp * 256:(hp + 1) * 256, :].rearrange("(b p) t -> p b t", p=128)[:, :, st * TS:(st + 1) * TS])
                    for jb in range(2):
                        ps = psf[jb]
                        mm(ps, ps.ap[:, 0:TS], [(W6.ap[:, k, jb * 128:(jb + 1) * 128], aA.ap[:, k, :]) for k in range(88)], [W6, aA])
                        V(lambda e, jb=jb, ps=ps: e.tensor_tensor(out=xo.ap[:, jb, :], in0=xt.ap[:, jb, :], in1=ps.ap[:, 0:TS], op=ALU.add),
                          [xt, ps], [xo])
                    dma(xcur, xcur.ap[hp * 256:(hp + 1) * 256, :].rearrange("(b p) t -> p b t", p=128)[:, :, st * TS:(st + 1) * TS], xo, xo.ap)
                stream_tiles([(aT, 88, aA2)], TS, body6)

            ckpt("s6")
          rmsnorm_gather(xcur, nf_t.ap[:, 0:4], final=True)
        except _Stop:
            pass
        if stop is not None:
            name, r0, r1, c0, c1 = dump
            src = dict(hT=hT[0], y_in=y_in[0], yT=yT[0], mg_in=mg_in[0], mgT=mgT[0], xmid=xmid, a_in=a_in[0], aT=aT[0],
                       xcur=xcur, h_in=h_in[0], ssq_all=ssq_all, ssq_in=ssq_in)[name]
            dbg = nc.dram_tensor("dbg", [r1 - r0, c1 - c0], src.ap.dtype, kind="ExternalOutput").ap()
            dma(Reg(dbg, multi=True), dbg, src, src.ap[r0:r1, c0:c1])
        P.barrier(final=True)
        P.emit(es)
    return nc


_NC_CACHE = {}


def _consts(r):
    idx = np.arange(128)
    ident = np.eye(128, dtype=np.float32)
    ones = np.ones((128, 128), np.float32)
    triT = (idx[:, None] <= idx[None, :]).astype(np.float32)
    negm = np.where(idx[:, None] <= idx[None, :], 0.0, -30000.0).astype(np.float32)
    Rm = np.zeros((128, 128), np.float32)
    Rm[(idx + 64) % 128, idx] = 1.0
    cst = np.stack([ident, ones, triT, negm, Rm], axis=1).astype(np.float32)
    half = 64
    inv = (np.float32(10000.0) ** (-np.arange(half, dtype=np.float32) / np.float32(half))).astype(np.float32)
    invf = np.concatenate([inv, inv])[:, None].astype(np.float32)
    scale = 128.0 ** -0.5
    retc = np.zeros((128, 516), np.float64)
    for hh in range(2):
        h = 2 * r + hh
        lg = np.log1p(-np.exp2(-5.0 - h))
        rel = idx[None, :] - idx[:, None]
        Dm = np.where(rel >= 0, np.exp(rel * lg), 0.0) * scale
        retc[:, hh * 128:(hh + 1) * 128] = Dm
        retc[:, 256 + hh * 128:256 + (hh + 1) * 128] = (np.exp((idx + 1.0) * lg) * scale)[None, :]
        retc[:, 512 + hh] = np.exp((127.0 - idx) * lg)
        retc[:, 514 + hh] = np.exp(128.0 * lg)
    gi = r // 2
    w = (2, 4, 8, 16)[gi]
    poolc = np.zeros((128, 1028), np.float32)
    poolc[:, gi] = 1.0
    t = np.arange(512)
    poolc[:, 4:516] = (1.0 / np.minimum(t + 1, w))[None, :]
    poolc[:, 516:1028] = 1.0 / w
    return cst, invf, retc.astype(np.float32), poolc


def _pp(v):
    return np.ascontiguousarray(v.reshape(-1, 128).T)


def _in_maps(x, positions, norm_mix, w_in, b_gate, conv_a_w, conv_a_b, dt_bias, a_log, d_skip, norm_a,
             pool_w, pool_scale, norm_c, w_br_a, w_br_b, w_br_c, w_out, norm_ffn, w_up, conv_f_w,
             conv_f_b, w_down, norm_f, DEPTH=DEPTH, only=None):
    f = lambda a: np.asarray(a)
    need = (lambda n: True) if only is None else (lambda n: n in only)
    x, positions = f(x), f(positions)
    if w_in is not None:
        w_in = f(w_in)
    if w_up is not None:
        w_up = f(w_up)
    if w_down is not None:
        w_down = f(w_down)
    xT = np.ascontiguousarray(x[0].T)
    posr = np.ascontiguousarray(np.broadcast_to(positions.reshape(1, L).astype(np.int32), (128, L)))
    in_maps = []
    for r in range(NCORES):
        cst, invf, retc, poolc = _consts(r)
        m = {"xs": np.ascontiguousarray(xT[r * 512:(r + 1) * 512]), "posr": posr, "cst": cst, "invf": invf,
             "retc": retc, "poolc": poolc, "nf": _pp(f(norm_f)[r * 512:(r + 1) * 512])}
        W1, WDT, W2, W2P, WPL, W3, PC, W4, W5, W6, PP, PB, PF = ([] for _ in range(13))
        gi = r // 2
        for i in range(DEPTH):
            wi = w_in[i] if w_in is not None else None
            cols = slice(r * 512, (r + 1) * 512)
            f0, f1 = r * FPC, min((r + 1) * FPC, DFF)
            nreal = f1 - f0
            if need("w1"):
                W1.append(np.concatenate([wi[:, r * 512:(r + 1) * 512], wi[:, 4096 + r * 512:4096 + (r + 1) * 512],
                                          wi[:, 8192 + r * 128:8192 + (r + 1) * 128],
                                          wi[:, 9216 + r * 128:9216 + (r + 1) * 128]], axis=1))
            if need("wdt"):
                WDT.append(wi[:, 10240 + r * 8:10240 + (r + 1) * 8])
            if need("w2"):
                W2.append(np.concatenate([wi[:, 12352 + r * 256:12352 + (r + 1) * 256],
                                          wi[:, 14400 + r * 256:14400 + (r + 1) * 256],
                                          wi[:, 16448 + r * 256:16448 + (r + 1) * 256],
                                          wi[:, 18496 + r * 256:18496 + (r + 1) * 256]], axis=1))
            if need("w2p"):
                W2P.append(wi[:, 10304 + gi * 512:10304 + (gi + 1) * 512])
            if need("wpool"):
                WPL.append(f(pool_w)[i, gi][:, (r % 2) * 256:(r % 2 + 1) * 256])
            if need("w3"):
                W3.append(np.concatenate([wi[:, 20544 + b * 4096 + r * 512:20544 + b * 4096 + (r + 1) * 512]
                                          for b in range(3)], axis=1))
            if need("pcat"):
                pa, pbm, pcm = f(w_br_a)[i][:, cols], f(w_br_b)[i][:, cols], f(w_br_c)[i][:, cols]
                PC.append(np.concatenate([np.concatenate([pa[q * 512:(q + 1) * 512], pbm[q * 256:(q + 1) * 256],
                                                          pcm[q * 256:(q + 1) * 256]], axis=0) for q in range(8)], axis=0))
            if need("w4"):
                W4.append(f(w_out)[i][:, cols])
            if need("w5"):
                w5 = np.zeros((D, 2 * FPC), np.float32)
                w5[:, :nreal] = w_up[i][:, f0:f1]
                w5[:, FPC:FPC + nreal] = w_up[i][:, DFF + f0:DFF + f1]
                W5.append(w5)
            if need("w6"):
                w6 = np.zeros((FPC * 8, 512), np.float32)
                wd = w_down[i][:, cols]
                for q in range(8):
                    g0, g1 = q * FPC, min((q + 1) * FPC, DFF)
                    w6[q * FPC:q * FPC + (g1 - g0)] = wd[g0:g1]
                W6.append(w6)
            pp = np.zeros((128, 64), np.float32)
            pp[:, 0:4] = _pp(f(norm_mix)[i][cols])
            pp[:, 4:8] = _pp(f(norm_ffn)[i][cols])
            caw, cab = f(conv_a_w)[i], f(conv_a_b)[i]
            chans = [np.arange(r * 512 + b * 128, r * 512 + (b + 1) * 128) for b in range(4)]
            chans.append(np.arange(4096 + r * 128, 4096 + (r + 1) * 128))
            chans.append(np.arange(5120 + r * 128, 5120 + (r + 1) * 128))
            for b, ch in enumerate(chans):
                for k in range(4):
                    pp[:, 8 + b * 4 + k] = caw[k, ch]
                pp[:, 32 + b] = cab[ch]
            pp[:, 38:40] = _pp(f(pool_scale)[i][r * 256:(r + 1) * 256])
            for b in range(3):
                pp[:, 40 + b * 4:44 + b * 4] = _pp(f(b_gate)[i][b * 4096 + r * 512:b * 4096 + (r + 1) * 512])
            PP.append(pp)
            pbv = np.concatenate([f(dt_bias)[i][r * 8:(r + 1) * 8], f(a_log)[i][r * 8:(r + 1) * 8],
                                  f(d_skip)[i][r * 8:(r + 1) * 8], f(norm_a)[i][cols], f(norm_c)[i]]).astype(np.float32)
            PB.append(np.ascontiguousarray(np.broadcast_to(pbv[None, :], (128, 664))))
            pfv = np.zeros((128, 88), np.float32)
            cfw, cfb = f(conv_f_w)[i], f(conv_f_b)[i]
            for jf in range(NJF):
                for ab in range(2):
                    feat = f0 + jf * 128 + np.arange(128)
                    ok = feat < DFF
                    ch = np.where(ok, feat, 0) + ab * DFF
                    for k in range(3):
                        pfv[:, (jf * 2 + ab) * 4 + k] = np.where(ok, cfw[k, ch], 0.0)
                    pfv[:, (jf * 2 + ab) * 4 + 3] = np.where(ok, cfb[ch], 0.0)
            PF.append(pfv)
        st = lambda lst: np.ascontiguousarray(np.stack(lst).astype(np.float32))
        for nm, lst in (("w1", W1), ("wdt", WDT), ("w2", W2), ("w2p", W2P), ("wpool", WPL), ("w3", W3), ("pcat", PC),
                        ("w4", W4), ("w5", W5), ("w6", W6), ("pp", PP), ("pb", PB), ("pf", PF)):
            if lst:
                m[nm] = st(lst)
        in_maps.append(m)
    return in_maps


def kernel(**inputs):
    in_maps = _in_maps(**inputs)
    if "nc" not in _NC_CACHE:
        _NC_CACHE["nc"] = build()
    nc = _NC_CACHE["nc"]
    in_maps = [{k: v for k, v in m.items() if k in nc._used_inputs} for m in in_maps]
    res = run_bass_kernel_spmd(nc, in_maps, core_ids=list(range(NCORES)))
    outT = np.concatenate([res.results[r]["out"] for r in range(NCORES)], axis=0)
    return np.ascontiguousarray(outT.T)[None].astype(np.float32)
```
